# Optimizing a Trainium2 kernel written in Bass

```python
import jax
import jax.numpy as jnp
from jax import lax
import numpy as np

D_MODEL = 1024
BATCH = 2
SEQ = 8192
DEPTH = 4
DEC_BATCH = 32
DEC_SEQ = 1
PAST_LEN = 8192
PAGE_SIZE = 128

N_HEADS = 8
HEAD_DIM = 64
D_ATTN = N_HEADS * HEAD_DIM
D_POOL = D_MODEL - D_ATTN
POOL_WINDOWS = (2, 4, 8, 16)
N_POOL_GROUPS = len(POOL_WINDOWS)
POOL_GROUP_DIM = D_POOL // N_POOL_GROUPS
POOL_BUF = max(POOL_WINDOWS) - 1
DILATION_PATTERNS = ((128, 1), (512, 4), (2048, 16))
MAX_WINDOW = max(w for w, _ in DILATION_PATTERNS)
BLOCK = 128
D_IN = 3 * D_ATTN + D_POOL
D_FF = 4 * D_MODEL
D_PLE = 256
EPS = 1e-6
NEG_INF = -1e30
ATTN_SCALE = HEAD_DIM ** -0.5

kernel_name = 'hymba_dilated_pool_decoder_step'


def rmsnorm(x, g):
    xf = x.astype(jnp.float32)
    y = xf * lax.rsqrt(jnp.mean(xf * xf, axis=-1, keepdims=True) + EPS)
    return (y * g.astype(jnp.float32)).astype(x.dtype)


def _project(xn, w_in):
    B, T, _ = xn.shape
    proj = xn @ w_in
    q = proj[..., :D_ATTN].reshape(B, T, N_HEADS, HEAD_DIM)
    k = proj[..., D_ATTN:2 * D_ATTN].reshape(B, T, N_HEADS, HEAD_DIM)
    v = proj[..., 2 * D_ATTN:3 * D_ATTN].reshape(B, T, N_HEADS, HEAD_DIM)
    u = proj[..., 3 * D_ATTN:]
    return q, k, v, u


def _dilated_attn_prompt(q, k, v, window, dil):
    B, S, H, E = q.shape
    span = window // dil
    lc = -(-S // (dil * BLOCK)) * BLOCK
    nb = lc // BLOCK
    pad = lc * dil - S

    def to_classes(t):
        t = jnp.pad(t, ((0, 0), (0, pad), (0, 0), (0, 0)))
        t = t.reshape(B, lc, dil, H, E).transpose(0, 2, 3, 1, 4)
        return t.reshape(B, dil, H, nb, BLOCK, E)

    def with_prev(t):
        prev = jnp.pad(t, ((0, 0), (0, 0), (0, 0), (1, 0), (0, 0), (0, 0)))[:, :, :, :nb]
        return jnp.concatenate([prev, t], axis=4)

    qb = to_classes(q)
    kb = with_prev(to_classes(k))
    vb = with_prev(to_classes(v))
    s = jnp.einsum('bdhnqe,bdhnke->bdhnqk', qb, kb).astype(jnp.float32) * ATTN_SCALE
    qi = jnp.arange(BLOCK)[:, None]
    kj = jnp.arange(2 * BLOCK)[None, :] - BLOCK
    rel = qi - kj
    key_idx = jnp.arange(nb)[:, None, None] * BLOCK + kj[None]
    mask = (rel >= 0) & (rel <= span) & (key_idx >= 0)
    s = jnp.where(mask, s, NEG_INF)
    m = jnp.max(s, axis=-1, keepdims=True)
    p = jnp.exp(s - m)
    den = jnp.sum(p, axis=-1)
    o = jnp.einsum('bdhnqk,bdhnke->bdhnqe', p, vb.astype(jnp.float32)) / den[..., None]
    lse = m[..., 0] + jnp.log(den)
    o = o.reshape(B, dil, H, lc, E).transpose(0, 3, 1, 2, 4).reshape(B, lc * dil, H, E)[:, :S]
    lse = lse.reshape(B, dil, H, lc).transpose(0, 3, 1, 2).reshape(B, lc * dil, H)[:, :S]
    return o, lse


def _dilated_attn_sample(q, k_ext, v_ext, window, dil):
    T = q.shape[1]
    n_hist = k_ext.shape[1] - T
    span = window // dil
    idx = (n_hist + jnp.arange(T))[:, None] - dil * jnp.arange(span + 1)[None, :]
    valid = idx >= 0
    idx = jnp.maximum(idx, 0)
    kg = k_ext[:, idx]
    vg = v_ext[:, idx]
    s = jnp.einsum('bthe,btkhe->bthk', q, kg).astype(jnp.float32) * ATTN_SCALE
    s = jnp.where(valid[None, :, None, :], s, NEG_INF)
    m = jnp.max(s, axis=-1, keepdims=True)
    p = jnp.exp(s - m)
    den = jnp.sum(p, axis=-1)
    o = jnp.einsum('bthk,btkhe->bthe', p, vg.astype(jnp.float32)) / den[..., None]
    lse = m[..., 0] + jnp.log(den)
    return o, lse


def _pool_mix(u_ext, n_new, pool_w, pool_scale):
    B, R, _ = u_ext.shape
    uf = u_ext.astype(jnp.float32)
    cs = jnp.pad(jnp.cumsum(uf, axis=1), ((0, 0), (1, 0), (0, 0)))
    rows = jnp.arange(R - n_new, R)
    cur = uf[:, R - n_new:]
    groups = []
    for g, w in enumerate(POOL_WINDOWS):
        sl = slice(g * POOL_GROUP_DIM, (g + 1) * POOL_GROUP_DIM)
        lo = jnp.maximum(rows + 1 - w, 0)
        cnt = (rows + 1 - lo).astype(jnp.float32)
        mean = (cs[:, rows + 1, sl] - cs[:, lo, sl]) / cnt[None, :, None]
        groups.append(mean - cur[..., sl])
    z = jnp.stack(groups, axis=2)
    y = jnp.einsum('bngc,gcd->bngd', z, pool_w.astype(jnp.float32))
    y = y.reshape(B, n_new, D_POOL) * pool_scale.astype(jnp.float32)
    return y.astype(u_ext.dtype)


def _mixer_out(outs, lses, pooled, w_out):
    wts = jax.nn.softmax(jnp.stack(lses, axis=0), axis=0)
    attn = jnp.einsum('gbth,gbthe->bthe', wts, jnp.stack(outs, axis=0))
    B, T = attn.shape[:2]
    attn = attn.reshape(B, T, D_ATTN).astype(pooled.dtype)
    return jnp.concatenate([attn, pooled], axis=-1) @ w_out


def _mlp(h, g, w_up, w_down):
    a = jax.nn.relu(rmsnorm(h, g) @ w_up)
    return (a * a) @ w_down


def _ple(h, p, g, w_gate, w_ple):
    gate = jax.nn.sigmoid(rmsnorm(h, g) @ w_gate)
    return (p @ w_ple) * gate


def setup_inputs(seed: int = 0) -> dict:
    key = jax.random.key(seed)
    ks = jax.random.split(key, 20)
    nrm = jax.random.normal
    w_buf = min(MAX_WINDOW, PAST_LEN)
    return {
        'x_prompt': nrm(ks[0], (BATCH, SEQ, D_MODEL), jnp.float32),
        'x_sample': nrm(ks[1], (DEC_BATCH, DEC_SEQ, D_MODEL), jnp.float32),
        'cache_attn_kv': nrm(ks[2], (DEPTH, DEC_BATCH, w_buf, 2, N_HEADS, HEAD_DIM), jnp.float32),
        'state_pool': nrm(ks[3], (DEPTH, DEC_BATCH, POOL_BUF, D_POOL), jnp.float32),
        'p_prompt': nrm(ks[4], (DEPTH, BATCH, SEQ, D_PLE), jnp.float32),
        'p_sample': nrm(ks[5], (DEPTH, DEC_BATCH, DEC_SEQ, D_PLE), jnp.float32),
        'norm_attn_g': 1.0 + 0.05 * nrm(ks[6], (DEPTH, D_MODEL), jnp.float32),
        'w_in': nrm(ks[7], (DEPTH, D_MODEL, D_IN), jnp.float32) * D_MODEL ** -0.5,
        'pool_w': nrm(ks[8], (DEPTH, N_POOL_GROUPS, POOL_GROUP_DIM, POOL_GROUP_DIM), jnp.float32) * POOL_GROUP_DIM ** -0.5,
        'pool_scale': 1.0 + 0.1 * nrm(ks[9], (DEPTH, D_POOL), jnp.float32),
        'w_out': nrm(ks[10], (DEPTH, D_MODEL, D_MODEL), jnp.float32) * D_MODEL ** -0.5,
        'norm_mlp_g': 1.0 + 0.05 * nrm(ks[11], (DEPTH, D_MODEL), jnp.float32),
        'w_up': nrm(ks[12], (DEPTH, D_MODEL, D_FF), jnp.float32) * D_MODEL ** -0.5,
        'w_down': nrm(ks[13], (DEPTH, D_FF, D_MODEL), jnp.float32) * D_FF ** -0.5,
        'ple_norm_g': 1.0 + 0.05 * nrm(ks[14], (DEPTH, D_MODEL), jnp.float32),
        'w_ple_gate': nrm(ks[15], (DEPTH, D_MODEL, D_MODEL), jnp.float32) * D_MODEL ** -0.5,
        'w_ple': nrm(ks[16], (DEPTH, D_PLE, D_MODEL), jnp.float32) * D_PLE ** -0.5,
        'final_norm_g': 1.0 + 0.05 * nrm(ks[17], (D_MODEL,), jnp.float32),
    }


def reference(x_prompt, x_sample, cache_attn_kv, state_pool, p_prompt, p_sample,
              norm_attn_g, w_in, pool_w, pool_scale, w_out, norm_mlp_g, w_up, w_down,
              ple_norm_g, w_ple_gate, w_ple, final_norm_g):
    hp = x_prompt
    hs = x_sample
    S = x_prompt.shape[1]
    T = x_sample.shape[1]
    kv_keep = min(MAX_WINDOW, S)
    kv_p_list, kv_s_list, pool_p_list, pool_s_list = [], [], [], []
    for i in range(DEPTH):
        q, k, v, u = _project(rmsnorm(hp, norm_attn_g[i]), w_in[i])
        outs, lses = [], []
        for window, dil in DILATION_PATTERNS:
            o, l = _dilated_attn_prompt(q, k, v, window, dil)
            outs.append(o)
            lses.append(l)
        pooled = _pool_mix(u, S, pool_w[i], pool_scale[i])
        hp = hp + _mixer_out(outs, lses, pooled, w_out[i])
        hp = hp + _mlp(hp, norm_mlp_g[i], w_up[i], w_down[i])
        hp = hp + _ple(hp, p_prompt[i], ple_norm_g[i], w_ple_gate[i], w_ple[i])
        kv_p_list.append(jnp.stack([k, v], axis=2)[:, S - kv_keep:])
        pool_p_list.append(u[:, S - POOL_BUF:])

        q, k, v, u = _project(rmsnorm(hs, norm_attn_g[i]), w_in[i])
        kv_c = cache_attn_kv[i].astype(k.dtype)
        k_ext = jnp.concatenate([kv_c[:, :, 0], k], axis=1)
        v_ext = jnp.concatenate([kv_c[:, :, 1], v], axis=1)
        outs, lses = [], []
        for window, dil in DILATION_PATTERNS:
            o, l = _dilated_attn_sample(q, k_ext, v_ext, window, dil)
            outs.append(o)
            lses.append(l)
        u_ext = jnp.concatenate([state_pool[i].astype(u.dtype), u], axis=1)
        pooled = _pool_mix(u_ext, T, pool_w[i], pool_scale[i])
        hs = hs + _mixer_out(outs, lses, pooled, w_out[i])
        hs = hs + _mlp(hs, norm_mlp_g[i], w_up[i], w_down[i])
        hs = hs + _ple(hs, p_sample[i], ple_norm_g[i], w_ple_gate[i], w_ple[i])
        kv_s_list.append(jnp.stack([k, v], axis=2))
        pool_s_list.append(u_ext[:, -POOL_BUF:])

    y_prompt = rmsnorm(hp, final_norm_g)
    y_sample = rmsnorm(hs, final_norm_g)
    kv_prompt = jnp.stack(kv_p_list, axis=0)
    kv_sample = jnp.stack(kv_s_list, axis=0)
    pool_prompt = jnp.stack(pool_p_list, axis=0)
    pool_sample = jnp.stack(pool_s_list, axis=0)
    return (y_prompt, y_sample, kv_prompt, kv_sample, pool_prompt, pool_sample)
```

```python
import math
import os
from contextlib import ExitStack
import numpy as np
import concourse.bass as bass
import concourse.mybir as mybir
from concourse.bass_utils import run_bass_kernel_spmd

F32 = mybir.dt.float32
BF16 = mybir.dt.bfloat16
I32 = mybir.dt.int32
AF = mybir.ActivationFunctionType
ALU = mybir.AluOpType
AX = mybir.AxisListType

NCORE = 8
L = int(os.environ.get('MK_L', '4'))
STAGES = os.environ.get('MK_STAGES', 'ABCDEFG')
SKIP = os.environ.get('MK_SKIP', '').split(',')
MKD = os.environ.get('MK_D', 'ltsevm')
D = 1024
NP = 2048
NS = 4
NT = NP + NS
TG = [(0, 512), (512, 512), (1024, 512), (1536, 512), (2048, 4)]
POOLW = (2, 4, 8, 16)
EPS = 1e-6
SCALE = 64 ** -0.5
NSLOT = {"sp": 16, "pool": 8, "act": 4}
NWSLOT = 5


class Op:
    __slots__ = ("eng", "fn", "dma", "inc", "deps", "needed", "sem", "val", "slot", "idx", "fenced")

    def __init__(self, eng, fn, dma, inc):
        self.eng = eng
        self.fn = fn
        self.dma = dma
        self.inc = inc
        self.deps = []
        self.needed = False
        self.sem = None
        self.val = 0
        self.slot = None
        self.fenced = True


class Sched:
    ENGS = ["pe", "act", "dve", "pool", "sp"]

    def __init__(self, nc):
        self.nc = nc
        self.ops = []
        self.lastw = {}
        self.readers = {}
        self.dma_rr = {q: 0 for q in NSLOT}
        self.dma_last = {}
        self.last_eng = {}
        self.unfenced_dma = []
        self.enabled = True

    def add(self, eng, fn, reads=(), writes=(), dma=False, inc=16, fenced=True):
        if not self.enabled:
            return None
        op = Op(eng, fn, dma, inc)
        op.fenced = fenced
        deps = []
        for k in reads:
            w = self.lastw.get(k)
            if w is not None:
                deps.append((w, "raw"))
        for k in writes:
            w = self.lastw.get(k)
            if w is not None:
                deps.append((w, "waw"))
            for r in self.readers.get(k, ()):
                deps.append((r, "war"))
        if dma:
            q = eng
            slot = (q, self.dma_rr[q] % NSLOT[q])
            self.dma_rr[q] += 1
            op.slot = slot
            prev = self.dma_last.get(slot)
            if prev is not None:
                deps.append((prev, "slot"))
            self.dma_last[slot] = op
        seen = set()
        for d, kind in deps:
            if d is op or id(d) in seen:
                continue
            if (not d.dma) and d.eng == eng and not dma:
                if eng == "pe" or kind != "raw":
                    continue
            seen.add(id(d))
            op.deps.append(d)
        for k in reads:
            self.readers.setdefault(k, []).append(op)
        for k in writes:
            self.lastw[k] = op
            self.readers[k] = []
        op.idx = len(self.ops)
        self.ops.append(op)
        if dma:
            if fenced:
                self.unfenced_dma.append(op)
        else:
            self.last_eng[eng] = op
        return op

    def fence(self):
        targets = list(self.last_eng.values()) + list(self.unfenced_dma)
        self.unfenced_dma = []
        for e in self.ENGS:
            op = Op(e, None, False, 1)
            op.deps = [t for t in targets if not ((not t.dma) and t.eng == e)]
            op.idx = len(self.ops)
            self.ops.append(op)

    def finish(self):
        op = Op("sp", None, False, 1)
        op.deps = [o for o in self.dma_last.values()]
        op.idx = len(self.ops)
        self.ops.append(op)

    def emit(self, stack):
        nc = self.nc
        for op in self.ops:
            for d in op.deps:
                d.needed = True
        engsem = {e: stack.enter_context(nc.semaphore("s_" + e)) for e in self.ENGS}
        slotsem = {}
        for q, n in NSLOT.items():
            for i in range(n):
                if (q, i) in self.dma_last:
                    slotsem[(q, i)] = stack.enter_context(nc.semaphore(f"d_{q}{i}"))
        cnt = {e: 0 for e in self.ENGS}
        scnt = {}
        for op in self.ops:
            if op.dma:
                scnt[op.slot] = scnt.get(op.slot, 0) + op.inc
                op.sem = slotsem[op.slot]
                op.val = scnt[op.slot]
            elif op.needed:
                cnt[op.eng] += 1
                op.sem = engsem[op.eng]
                op.val = cnt[op.eng]
        per = {e: [o for o in self.ops if o.eng == e] for e in self.ENGS}
        self.stats = {e: len(per[e]) for e in self.ENGS}
        self.stats["maxcnt"] = dict(cnt)
        block = stack.enter_context(nc.Block())
        handles = {"pe": "tensor", "act": "scalar", "dve": "vector", "pool": "gpsimd", "sp": "sync"}

        def make(e):
            def body(eng):
                waited = {}
                for op in per[e]:
                    for d in op.deps:
                        key = id(d.sem)
                        if waited.get(key, 0) >= d.val:
                            continue
                        waited[key] = d.val
                        eng.wait_ge(d.sem, d.val)
                    if op.fn is None:
                        continue
                    ins = op.fn(eng)
                    if op.dma:
                        ins.then_inc(op.sem, op.inc)
                    elif op.needed:
                        ins.then_inc(op.sem, 1)

            return body

        for e in self.ENGS:
            if per[e]:
                getattr(block, handles[e])(make(e))


def build():
    nc = bass.Bass("TRN2", target_bir_lowering=False)

    def din(name, shape, dt=F32):
        return nc.dram_tensor(name, shape, dt, kind="ExternalInput").ap()

    def dout(name, shape, dt=F32):
        return nc.dram_tensor(name, shape, dt, kind="ExternalOutput").ap()

    def dscr(name, shape, dt):
        return nc.dram_tensor(name, shape, dt).ap()

    xT = din("xT", [D, NT])
    pT = din("pT", [L, 256, NT])
    cache = din("cache", [L, NS, 2048, 1024])
    state = din("state", [L, NS, 15, 512])
    w_in = din("w_in", [L, D, 2048])
    pool_w = din("pool_w", [L, 4, 128, 128])
    w_out = din("w_out", [L, D, D])
    w_up = din("w_up", [L, D, 4096])
    w_down = din("w_down", [L, 4096, D])
    w_gate = din("w_gate", [L, D, D])
    w_ple = din("w_ple", [L, 256, D])
    vecs = din("vecs", [128, 120])
    masks = din("masks", [128, 4, 512])
    cst = din("cst", [128, 80])
    smallc = din("smallc", [16, 548])
    idx = din("idx", [128, 8], I32)
    idxu = din("idxu", [128, 4], I32)
    ident = din("ident", [128, 128])

    yT = dout("yT", [D, NT])
    kvT = dout("kvT", [L, 1024, NP])
    kvs = dout("kvs", [L, NS, 2, 512])
    poolp = dout("poolp", [L, 512, 16])
    pools = dout("pools", [L, NS, 15, 512])

    qscr = [dscr(f"qscr{l}", [512, NP], BF16) for l in range(L)]
    exin = [dscr(f"exin{l}", [1024, NP], BF16) for l in range(L)]
    exout = [dscr(f"exout{l}", [NCORE * 1024, NP], BF16) for l in range(L)]
    utin = [dscr(f"utin{l}", [512, 16], F32) for l in range(L)]
    utout = [dscr(f"utout{l}", [NCORE * 512, 16], F32) for l in range(L)]

    st = ExitStack()
    with st:
        def sb(name, shape, dt):
            return st.enter_context(nc.sbuf_tensor(name, shape, dt))

        hT = sb("hT", [128, 8, NT], F32)
        xn = sb("xn", [128, 8, NT], BF16)
        wring = sb("wring", [128, NWSLOT * 4096], BF16)
        ARENA_W = 60 * 256
        arena = sb("arena", [128, ARENA_W], F32)
        masks_bf = sb("masks_bf", [128, 4, 512], BF16)
        ident_bf = sb("ident_bf", [128, 128], BF16)
        ones_bf = sb("ones_bf", [128, 128], BF16)
        ones_f = sb("ones_f", [128, 128], F32)
        vecs_sb = sb("vecs_sb", [128, 120], F32)
        cst_sb = sb("cst_sb", [128, 80], F32)
        smallc_sb = sb("smallc_sb", [16, 548], F32)
        idx_sb = sb("idx_sb", [128, 8], I32)
        idxu_sb = sb("idxu_sb", [128, 4], I32)
        poolw_bf = sb("poolw_bf", [128, 4, 128], BF16)
        rs = sb("rs", [128, 512], F32)
        eps_sb = sb("eps_sb", [128, 1], F32)
        pbs = [st.enter_context(nc.psum_tensor(f"pb{i}", [128, 512], F32)) for i in range(8)]

        S = Sched(nc)

        def aview(off_b, shape, dt, parts=128):
            esz = 2 if dt == BF16 else 4
            n = 1
            for s_ in shape[1:]:
                n *= s_
            nbytes = n * esz
            assert off_b % 4 == 0 and nbytes % 4 == 0 and off_b + nbytes <= ARENA_W * 4, (off_b, shape)
            ap = arena[0:parts, off_b // 4:(off_b + nbytes) // 4]
            if dt != F32:
                ap = ap.bitcast(dt)
            if len(shape) == 3:
                ap = ap.rearrange("p (a b) -> p a b", a=shape[1])
            elif len(shape) == 4:
                ap = ap.rearrange("p (a b c) -> p a b c", a=shape[1], b=shape[2])
            return ap

        KB = 1024

        def mm(out, lhsT, rhs, start, stop, reads, writes):
            S.add("pe", lambda e, o=out, l_=lhsT, r=rhs, s_=start, t=stop: e.matmul(o, lhsT=l_, rhs=r, start=s_, stop=t),
                  reads=reads, writes=writes)

        def act(out, in_, func, reads, writes, scale=1.0, bias=None):
            if bias is None:
                S.add("act", lambda e, o=out, i=in_, f=func, sc=scale: e.activation(out=o, in_=i, func=f, scale=sc),
                      reads=reads, writes=writes)
            else:
                S.add("act", lambda e, o=out, i=in_, f=func, sc=scale, b=bias: e.activation(out=o, in_=i, func=f, scale=sc, bias=b),
                      reads=reads, writes=writes)

        def tt(eng, out, in0, in1, op, reads, writes):
            S.add(eng, lambda e, o=out, a=in0, b=in1, p=op: e.tensor_tensor(out=o, in0=a, in1=b, op=p), reads=reads, writes=writes)

        def stt(eng, out, in0, scalar, in1, op0, op1, reads, writes):
            S.add(eng, lambda e, o=out, a=in0, s_=scalar, b=in1, p0=op0, p1=op1: e.scalar_tensor_tensor(out=o, in0=a, scalar=s_, in1=b, op0=p0, op1=p1),
                  reads=reads, writes=writes)

        def ts(eng, out, in0, s1, s2, op0, op1, reads, writes):
            if op1 is None:
                S.add(eng, lambda e, o=out, a=in0, x=s1, p0=op0: e.tensor_scalar(out=o, in0=a, scalar1=x, scalar2=None, op0=p0), reads=reads, writes=writes)
            else:
                S.add(eng, lambda e, o=out, a=in0, x=s1, y=s2, p0=op0, p1=op1: e.tensor_scalar(out=o, in0=a, scalar1=x, scalar2=y, op0=p0, op1=p1),
                      reads=reads, writes=writes)

        def cp(eng, out, in_, reads, writes):
            if eng == "act":
                S.add(eng, lambda e, o=out, i=in_: e.activation(out=o, in_=i, func=AF.Copy), reads=reads, writes=writes)
            else:
                S.add(eng, lambda e, o=out, i=in_: e.tensor_copy(out=o, in_=i), reads=reads, writes=writes)

        def dma(q, out, in_, reads, writes, fenced=True):
            nm = str(getattr(getattr(out, "tensor", None), "name", ""))
            if any(sk and nm.startswith(sk) for sk in SKIP):
                return
            S.add(q, lambda e, o=out, i=in_: e.dma_start(out=o, in_=i), reads=reads, writes=writes, dma=True, fenced=fenced)

        wstate = {"pos": 0, "n": 0}

        def walloc(nslots):
            if wstate["pos"] + nslots > NWSLOT:
                wstate["pos"] = 0
            s0 = wstate["pos"]
            wstate["pos"] += nslots
            keys = [("w", s0 + i) for i in range(nslots)]
            return wring[:, s0 * 4096:(s0 + nslots) * 4096], keys

        def wload(src_ap, nslots, kdim):
            wv, keys = walloc(nslots)
            mdim = src_ap.shape[-1]
            wv3 = wv[:, 0:kdim * mdim].rearrange("p (k m) -> p k m", k=kdim)
            dma("pool", wv3, src_ap.rearrange("(k p) m -> p k m", p=128), reads=[], writes=keys, fenced=False)
            return wv3, keys

        pbrr = {"i": 0}

        def nextbank(lo=0, hi=6):
            b = lo + pbrr["i"] % (hi - lo)
            pbrr["i"] += 1
            return b

        dma("sp", vecs_sb[:], vecs, [], ["vecs"])
        dma("sp", cst_sb[:], cst, [], ["cst"])
        dma("sp", smallc_sb[:], smallc, [], ["smallc"])
        dma("sp", idx_sb[:], idx, [], ["idx"])
        dma("sp", idxu_sb[:], idxu, [], ["idxu"])
        dma("pool", masks_bf[:], masks, [], ["masks"])
        dma("pool", ident_bf[:], ident, [], ["ident"])
        S.add("pool", lambda e: e.memset(ones_bf[:], 1.0), writes=["ones_bf"])
        S.add("pool", lambda e: e.memset(ones_f[:], 1.0), writes=["ones_f"])
        S.add("pool", lambda e: e.memset(eps_sb[:], EPS), writes=["eps"])
        for kc in range(8):
            dma("sp", hT[:, kc, :], xT[kc * 128:(kc + 1) * 128, :], [], [("hT", kc, g) for g in range(5)])

        def vcol(c):
            return vecs_sb[:, c:c + 1]

        SQ_OFF = 44032
        stokT = aview(53248, [NS, 4, 512], F32, parts=NS)

        def rmsnorm(gbase, out_fn):
            sqt = aview(SQ_OFF, [128, 8, 512], BF16)
            for g, (c0, n) in enumerate(TG):
                act(sqt[:, :, 0:n], hT[:, :, c0:c0 + n], AF.Square, reads=[("hT", kc, g) for kc in range(8)], writes=["sqt"])
                for kc in range(8):
                    mm(pbs[6][:, 0:n], ones_bf[:], sqt[:, kc, 0:n], kc == 0, kc == 7, ["sqt", "ones_bf"], [("pb", 6)])
                act(rs[:, 0:n], pbs[6][:, 0:n], AF.Sqrt, [("pb", 6), "eps"], ["rs"], scale=1.0 / D, bias=eps_sb[:, 0:1])
                S.add("dve", lambda e, o=rs[:, 0:n], i=rs[:, 0:n]: e.reciprocal(out=o, in_=i), reads=["rs"], writes=["rs"])
                for kc in range(8):
                    out_fn(kc, g, c0, n, vcol(gbase + kc))

        def norm_to_xn(gbase):
            def f(kc, g, c0, n, gcol):
                stt("dve", xn[:, kc, c0:c0 + n], hT[:, kc, c0:c0 + n], gcol, rs[:, 0:n], ALU.mult, ALU.mult,
                    [("hT", kc, g), "rs", "vecs"], [("xn", kc, g)])
            rmsnorm(gbase, f)

        for l in range(L):
            vb = l * 28
            S.enabled = ('A' in STAGES)
            S.fence()
            norm_to_xn(vb)
            uT = aview(0, [128, 4, 2068], F32)
            stg = [aview(33792 + i * 4096, [128, 2048], BF16) for i in range(2)]
            stgf = [aview(41984, [128, 512], F32) for i in range(1)]
            poolw_keys = ["poolw"]
            dma("pool", poolw_bf[:], pool_w[l].rearrange("g c d -> c g d"), [], poolw_keys, fenced=False)
            stgi = 0
            stfi = 0
            A_ORDER = (3, 1, 2, 0)
            wtiles = {s4: wload(w_in[l][:, s4 * 512:(s4 + 1) * 512], 1, 8) for s4 in A_ORDER}
            for s4 in A_ORDER:
                wv, wk = wtiles[s4]
                S.enabled = ('A' in STAGES) and ('stok' not in SKIP)
                bnk = nextbank()
                for kc in range(8):
                    mm(pbs[bnk][0:NS, :], xn[:, kc, NP:NT], wv[:, kc, :], kc == 0, kc == 7,
                       [("xn", kc, 4)] + wk, [("pb", bnk)])
                cp("act", stokT[0:NS, s4, :], pbs[bnk][0:NS, :], [("pb", bnk)], [("stok", s4)])
                S.enabled = ('A' in STAGES)
                for mc4 in range(4):
                    mc = s4 * 4 + mc4
                    ngr = 5 if s4 == 3 else 4
                    if s4 < 3:
                        sg = stg[stgi % 2]
                        sgk = ("stg", stgi % 2)
                        stgi += 1
                    for g in range(ngr):
                        c0, n = TG[g]
                        bnk = nextbank()
                        for kc in range(8):
                            mm(pbs[bnk][:, 0:n], wv[:, kc, mc4 * 128:(mc4 + 1) * 128], xn[:, kc, c0:c0 + n], kc == 0, kc == 7,
                               [("xn", kc, g)] + wk, [("pb", bnk)])
                        if s4 == 3:
                            cp("act", uT[:, mc4, 16 + c0:16 + c0 + n], pbs[bnk][:, 0:n], [("pb", bnk)], [("uT", mc4, g)])
                        elif s4 == 0:
                            cp("act", sg[:, c0:c0 + n], pbs[bnk][:, 0:n], [("pb", bnk)], [sgk])
                        else:
                            if True:
                                sf = stgf[0]
                                sfk = ("stgf", 0)
                                stfi += 1
                                cp("act", sf[:, 0:n], pbs[bnk][:, 0:n], [("pb", bnk)], [sfk])
                                cp("dve", sg[:, c0:c0 + n], sf[:, 0:n], [sfk], [sgk])
                                row0 = (s4 - 1) * 512 + mc4 * 128
                                dma("sp", kvT[l][row0:row0 + 128, c0:c0 + n], sf[:, 0:n], [sfk], [])
                    if s4 == 0:
                        dma("sp", qscr[l][mc4 * 128:(mc4 + 1) * 128, :], sg[:, :], [sgk], [("qscr", mc4)])
                    elif s4 < 3:
                        row0 = (s4 - 1) * 512 + mc4 * 128
                        dma("sp", exin[l][row0:row0 + 128, :], sg[:, :], [sgk], [("exin", row0 // 128)])
                if s4 == 2:
                    S.add("pool", lambda e, a=exin[l], b=exout[l]: e.collective_compute(
                        "AllGather", ALU.bypass, replica_groups=[list(range(NCORE))], ins=[a.opt()], outs=[b.opt()]),
                        reads=[("exin", j) for j in range(8)], writes=["exout"], dma=True, inc=1, fenced=False)
                if s4 == 3:
                    for g4 in range(4):
                        dma("sp", utin[l][g4 * 128:(g4 + 1) * 128, :], uT[:, g4, 16 + NP - 16:16 + NP], [("uT", g4, 3)], [("utin", g4)])
                        dma("sp", poolp[l][g4 * 128:(g4 + 1) * 128, :], uT[:, g4, 16 + NP - 16:16 + NP], [("uT", g4, 3)], [])
                    S.add("pool", lambda e, a=utin[l], b=utout[l]: e.collective_compute(
                        "AllGather", ALU.bypass, replica_groups=[list(range(NCORE))], ins=[a.opt()], outs=[b.opt()]),
                        reads=[("utin", j) for j in range(4)], writes=["utout"], dma=True, inc=1)
            dma("sp", kvs[l][:, 0, :], stokT[0:NS, 1, :], [("stok", 1)], [])
            dma("sp", kvs[l][:, 1, :], stokT[0:NS, 2, :], [("stok", 2)], [])
            dma("sp", pools[l][:, 14, :], stokT[0:NS, 3, :], [("stok", 3)], [])
            if 'd2d' not in SKIP: dma("sp", pools[l][:, 0:14, :], state[l][:, 1:15, :], [], [])

            S.enabled = ('B' in STAGES)
            uh = aview(33792, [128, 4, 16], F32)
            z16 = aview(34048, [128, 16], F32)
            tA = aview(34304, [128, 2064], F32)
            tB = aview(42560, [128, 2064], F32)
            zA = aview(34304, [128, NT], BF16)
            zB = aview(42560, [128, NT], BF16)
            state_g = aview(50816, [15, NS, 128], F32, parts=15)
            S.fence()
            for g4 in range(4):
                S.add("pool", lambda e, o=uh[:, g4, :], src=utout[l], ix=idxu_sb[:, g4:g4 + 1]: e.indirect_dma_start(
                    out=o, out_offset=None, in_=src, in_offset=bass.IndirectOffsetOnAxis(ap=ix, axis=0)),
                    reads=["utout", "idxu"], writes=[("uh", g4)], dma=True)
            for g4 in range(4):
                w = POOLW[g4]
                X = uT[:, g4, :]
                tt("dve", uT[:, g4, 0:16], uh[:, g4, :], cst_sb[:, 64:80], ALU.mult, [("uh", g4), "cst"], [("uTh", g4)])
                allu = [("uT", g4, g) for g in range(5)] + [("uTh", g4)]
                tt("pool", tA[:, 1:2064], X[:, 1:2064], X[:, 0:2063], ALU.add, allu, ["tA"])
                fin, fk = tA, "tA"
                if w >= 4:
                    tt("pool", tB[:, 3:2064], tA[:, 3:2064], tA[:, 1:2062], ALU.add, ["tA"], ["tB"])
                    fin, fk = tB, "tB"
                if w >= 8:
                    tt("pool", tA[:, 7:2064], tB[:, 7:2064], tB[:, 3:2060], ALU.add, ["tB"], ["tA"])
                    fin, fk = tA, "tA"
                if w >= 16:
                    tt("pool", tB[:, 15:2064], tA[:, 15:2064], tA[:, 7:2056], ALU.add, ["tA"], ["tB"])
                    fin, fk = tB, "tB"
                z, zk = (zB, "tB") if fk == "tA" else (zA, "tA")
                dma("sp", state_g[:, :, :], state[l][:, :, g4 * 128:(g4 + 1) * 128].rearrange("b r f -> r b f"), [], ["state_g"])
                stt("dve", z[:, 16:NP], fin[:, 32:2064], 1.0 / w, X[:, 32:2064], ALU.mult, ALU.subtract, [fk] + allu, [zk])
                tt("dve", z16[:, :], fin[:, 16:32], cst_sb[:, g4 * 16:(g4 + 1) * 16], ALU.mult, [fk, "cst"], ["z16"])
                tt("dve", z[:, 0:16], z16[:, :], X[:, 16:32], ALU.subtract, ["z16"] + allu, [zk])
                for b in range(NS):
                    mm(pbs[6][:, g4 * 4 + b:g4 * 4 + b + 1], state_g[0:15, b, :],
                       smallc_sb[0:15, 544 + g4:545 + g4], True, True, ["state_g", "smallc"], [("pb", 6)])
                stt("dve", z[:, NP:NT], X[:, 16 + NP:16 + NT], 1.0 / w - 1.0, pbs[6][:, g4 * 4:g4 * 4 + 4], ALU.mult, ALU.add,
                    [("pb", 6)] + allu, [zk])
                for g, (c0, n) in enumerate(TG):
                    bnk = nextbank()
                    mm(pbs[bnk][:, 0:n], poolw_bf[:, g4, :], z[:, c0:c0 + n], True, True, [zk, "poolw"], [("pb", bnk)])
                    S.add("act", lambda e, o=xn[:, 4 + g4, c0:c0 + n], i=pbs[bnk][:, 0:n], sc=vcol(vb + 24 + g4): e.activation(
                        out=o, in_=i, func=AF.Copy, scale=sc), reads=[("pb", bnk), "vecs"], writes=[("xn", 4 + g4, g)])

            S.enabled = ('C' in STAGES)
            S.fence()
            KVg = [aview(i * 4096, [128, 1024], F32) for i in range(6)]
            tmp = aview(24 * KB, [128, 512], F32)
            sc_t = aview(26624, [128, 12, 8], F32)
            pS_t = aview(27008, [128, 12, 8], F32)
            rden = aview(27392, [128, 4], F32)
            prod_s = aview(27648, [NS, 512], F32, parts=NS)
            ss_t = aview(29696, [NS, 8], F32, parts=NS)
            pself = aview(29728, [NS, 8], F32, parts=NS)
            pselfm = aview(29760, [NS, NS, 8], F32, parts=NS)
            tt("dve", prod_s[:, :], stokT[0:NS, 0, :], stokT[0:NS, 1, :], ALU.mult, [("stok", 0), ("stok", 1)], ["prod_s"])
            S.add("dve", lambda e, o=ss_t[:, :], i=prod_s.rearrange("p (h e) -> p h e", h=8): e.tensor_reduce(
                out=o, in_=i, axis=AX.X, op=ALU.add), reads=["prod_s"], writes=["ss"])
            act(pself[:, :], ss_t[:, :], AF.Exp, ["ss"], ["pself"], scale=SCALE)
            for b in range(NS):
                tt("dve", pselfm[:, b, :], pself[:, :], smallc_sb[0:NS, b * 8:(b + 1) * 8], ALU.mult, ["pself", "smallc"], [("pselfm", b)])
            kvi = 0
            for b in range(NS):
                mm(pbs[0][:, :], smallc_sb[0:NS, 32 + b * 128:32 + (b + 1) * 128], stokT[0:NS, 0, :], True, True,
                   [("stok", 0), "smallc"], [("pb", 0)])
                kvb = []
                for pi, dil in enumerate((1, 4, 16)):
                    kg = KVg[kvi % 6]
                    kk = ("kvg", kvi % 6)
                    kvi += 1
                    kvb.append((kg, kk))
                    r0 = 2048 - 128 * dil
                    dma("sp", kg[:, :], cache[l][b, r0:2048:dil, :], [], [kk])
                    tt("dve", tmp[:, :], kg[:, 0:512], pbs[0][:, :], ALU.mult, [kk, ("pb", 0)], ["tmp"])
                    S.add("dve", lambda e, o=sc_t[:, b * 3 + pi, :], i=tmp.rearrange("p (h e) -> p h e", h=8): e.tensor_reduce(
                        out=o, in_=i, axis=AX.X, op=ALU.add), reads=["tmp"], writes=[("sc", b)])
                act(pS_t[:, b * 3:b * 3 + 3, :], sc_t[:, b * 3:b * 3 + 3, :], AF.Exp, [("sc", b)], [("pS", b)], scale=SCALE)
                for c in range(4):
                    for pi in range(3):
                        kg, kk = kvb[pi]
                        mm(pbs[1][:, c * 8:(c + 1) * 8], kg[:, 512 + c * 128:512 + (c + 1) * 128], pS_t[:, b * 3 + pi, :],
                           pi == 0, False, [kk, ("pS", b)], [("pb", 1)])
                    mm(pbs[1][:, c * 8:(c + 1) * 8], stokT[0:NS, 2, c * 128:(c + 1) * 128], pselfm[:, b, :], False, True,
                       [("stok", 2), ("pselfm", b)], [("pb", 1)])
                for pi in range(3):
                    mm(pbs[1][:, 32:40], ones_f[:, :], pS_t[:, b * 3 + pi, :], pi == 0, False, [("pS", b), "ones_f"], [("pb", 1)])
                mm(pbs[1][:, 32:40], ones_f[0:NS, :], pselfm[:, b, :], False, True, [("pselfm", b), "ones_f"], [("pb", 1)])
                S.add("dve", lambda e, o=rden[0:64, :], i=pbs[1][0:64, 32:40:2]: e.reciprocal(out=o, in_=i), reads=[("pb", 1)], writes=["rden0"])
                S.add("dve", lambda e, o=rden[64:128, :], i=pbs[1][64:128, 33:40:2]: e.reciprocal(out=o, in_=i), reads=[("pb", 1)], writes=["rden1"])
                tt("dve", xn[0:64, 0:4, NP + b], pbs[1][0:64, 0:31:10], rden[0:64, :], ALU.mult, [("pb", 1), "rden0"], [("xn", c_, 4) for c_ in range(4)])
                tt("dve", xn[64:128, 0:4, NP + b], pbs[1][64:128, 1:32:10], rden[64:128, :], ALU.mult, [("pb", 1), "rden1"], [("xn", c_, 4) for c_ in range(4)])

            S.enabled = ('D' in STAGES)
            S.fence()
            qTp = aview(0, [128, NP], BF16)
            kvp = aview(4 * KB, [128, 2, 2 * NP], BF16)
            Vt = aview(20 * KB, [128, 32, 128], BF16)
            accN = aview(28 * KB, [128, NP], F32)
            accD = aview(36 * KB, [128, NP], F32)
            Pt = [[aview(44 * KB + (hh * 2 + ab) * 1024, [128, 512], BF16) for ab in range(2)] for hh in range(2)]
            for hp in range(4):
                S.enabled = ('D' in STAGES) and ('l' in MKD)
                dma("sp", qTp[:, :], qscr[l][hp * 128:(hp + 1) * 128, :], [("qscr", hp)], ["qTp"])
                dma("sp", kvp[:, 0, NP:2 * NP], exin[l][hp * 128:(hp + 1) * 128, :], [("exin", hp)], [("kvp_k", 1)])
                dma("sp", kvp[:, 1, NP:2 * NP], exin[l][512 + hp * 128:512 + (hp + 1) * 128, :], [("exin", 4 + hp)], [("kvp_v", 1)])
                S.add("pool", lambda e, o=kvp[:, 0, 0:NP], src=exout[l], ix=idx_sb[:, hp:hp + 1]: e.indirect_dma_start(
                    out=o, out_offset=None, in_=src, in_offset=bass.IndirectOffsetOnAxis(ap=ix, axis=0)),
                    reads=["exout", "idx"], writes=[("kvp_k", 0)], dma=True)
                S.add("pool", lambda e, o=kvp[:, 1, 0:NP], src=exout[l], ix=idx_sb[:, 4 + hp:5 + hp]: e.indirect_dma_start(
                    out=o, out_offset=None, in_=src, in_offset=bass.IndirectOffsetOnAxis(ap=ix, axis=0)),
                    reads=["exout", "idx"], writes=[("kvp_v", 0)], dma=True)
                S.enabled = ('D' in STAGES)
                for pi, dil in enumerate((1, 4, 16)):
                    if dil == 1:
                        groups = [[(0, m) for m in range(4 * G, 4 * G + 4)] for G in range(4)]
                    elif dil == 4:
                        groups = [[(G, m) for m in range(4)] for G in range(4)]
                    else:
                        groups = [[(r, 0) for r in range(4 * G, 4 * G + 4)] for G in range(4)]

                    def tok(r, m, _d=dil):
                        return NP + m * 128 * _d + r

                    tiles = {}
                    tl = []
                    for grp in groups:
                        for (r, m) in grp:
                            tl.append((r, m))
                            if m == 0:
                                tl.append((r, -1))
                    for ti, (r, m) in enumerate(tl):
                        tiles[(r, m)] = ti
                    S.enabled = ('D' in STAGES) and ('t' in MKD)
                    for t0 in range(0, len(tl), 4):
                        chunk = tl[t0:t0 + 4]
                        bk = 6 + (t0 // 4) % 2
                        for j, (r, m) in enumerate(chunk):
                            s0 = tok(r, m)
                            mm(pbs[bk][:, j * 128:(j + 1) * 128], kvp[:, 1, s0:s0 + 127 * dil + 1:dil], ident_bf[:], True, True,
                               [("kvp_v", 0), ("kvp_v", 1), "ident"], [("pb", bk)])
                        nn = len(chunk)
                        cp("act", Vt[:, t0:t0 + nn, :], pbs[bk][:, 0:nn * 128].rearrange("p (a b) -> p a b", a=nn), [("pb", bk)], [("Vt", t0 // 4)])
                    vt_reads = [("Vt", i) for i in range((len(tl) + 3) // 4)]
                    S.enabled = ('D' in STAGES)
                    def d_masks(G, _d=dil):
                        if _d == 1:
                            return masks_bf[:, 1 if G == 0 else 0, :], masks_bf[:, 3, :]
                        if _d == 4:
                            return masks_bf[:, 1, :], masks_bf[:, 3, :]
                        return masks_bf[:, 2, :], masks_bf[:, 3, :]

                    def d_scores(G, hh, _d=dil):
                        po = hh * 64
                        ba, bb = (0, 1) if hh == 0 else (2, 3)
                        for bi, (r, m) in enumerate(groups[G]):
                            qs = m * 128 * _d + r
                            qap = qTp[po:po + 64, qs:qs + 127 * _d + 1:_d]
                            s_prev = tok(r, m - 1)
                            s_own = tok(r, m)
                            mm(pbs[ba][:, bi * 128:(bi + 1) * 128], kvp[po:po + 64, 0, s_prev:s_prev + 127 * _d + 1:_d], qap, True, True,
                               [("kvp_k", 0), ("kvp_k", 1), "qTp"], [("pb", ba)])
                            mm(pbs[bb][:, bi * 128:(bi + 1) * 128], kvp[po:po + 64, 0, s_own:s_own + 127 * _d + 1:_d], qap, True, True,
                               [("kvp_k", 0), ("kvp_k", 1), "qTp"], [("pb", bb)])

                    def d_softmax(G, hh):
                        ba, bb = (0, 1) if hh == 0 else (2, 3)
                        mA, mB = d_masks(G)
                        Pa, Pb = Pt[hh]
                        act(Pa[:, :], pbs[ba][:, :], AF.Exp, [("pb", ba)], [("P", hh, 0)], scale=SCALE)
                        act(Pb[:, :], pbs[bb][:, :], AF.Exp, [("pb", bb)], [("P", hh, 1)], scale=SCALE)
                        tt("dve", Pa[:, :], Pa[:, :], mA, ALU.mult, [("P", hh, 0), "masks"], [("P", hh, 0)])
                        tt("dve", Pb[:, :], Pb[:, :], mB, ALU.mult, [("P", hh, 1), "masks"], [("P", hh, 1)])

                    def d_pv(G, hh):
                        po = hh * 64
                        bN, bD = (4, 5) if G % 2 == 0 else (6, 7)
                        Pa, Pb = Pt[hh]
                        for bi, (r, m) in enumerate(groups[G]):
                            tp = tiles[(r, m - 1)]
                            to = tiles[(r, m)]
                            mm(pbs[bN][po:po + 64, bi * 128:(bi + 1) * 128], Vt[:, tp, po:po + 64], Pa[:, bi * 128:(bi + 1) * 128], True, False,
                               vt_reads + [("P", hh, 0)], [("pb", bN)])
                            mm(pbs[bN][po:po + 64, bi * 128:(bi + 1) * 128], Vt[:, to, po:po + 64], Pb[:, bi * 128:(bi + 1) * 128], False, True,
                               vt_reads + [("P", hh, 1)], [("pb", bN)])
                        mm(pbs[bD][po:po + 64, :], ones_bf[:, 0:64], Pa[:, :], True, False, [("P", hh, 0), "ones_bf"], [("pb", bD)])
                        mm(pbs[bD][po:po + 64, :], ones_bf[:, 0:64], Pb[:, :], False, True, [("P", hh, 1), "ones_bf"], [("pb", bD)])

                    def d_merge(G, _d=dil, _pi=pi):
                        bN, bD = (4, 5) if G % 2 == 0 else (6, 7)
                        if _d == 1:
                            dN = accN[:, G * 512:(G + 1) * 512]
                            dD = accD[:, G * 512:(G + 1) * 512]
                            sN, sD = pbs[bN][:, :], pbs[bD][:, :]
                        elif _d == 4:
                            dN = accN.rearrange("p (i r) -> p r i", r=4)[:, G, :]
                            dD = accD.rearrange("p (i r) -> p r i", r=4)[:, G, :]
                            sN, sD = pbs[bN][:, :], pbs[bD][:, :]
                        else:
                            dN = accN.rearrange("p (i r) -> p r i", r=16)[:, 4 * G:4 * G + 4, :]
                            dD = accD.rearrange("p (i r) -> p r i", r=16)[:, 4 * G:4 * G + 4, :]
                            sN = pbs[bN][:, :].rearrange("p (a b) -> p a b", a=4)
                            sD = pbs[bD][:, :].rearrange("p (a b) -> p a b", a=4)
                        if _pi == 0:
                            ts("dve", dN, sN, 1.0, None, ALU.mult, None, [("pb", bN)], ["accN"])
                            ts("dve", dD, sD, 1.0, None, ALU.mult, None, [("pb", bD)], ["accD"])
                        else:
                            tt("dve", dN, dN, sN, ALU.add, [("pb", bN), "accN"], ["accN"])
                            tt("dve", dD, dD, sD, ALU.add, [("pb", bD), "accD"], ["accD"])

                    units = [(G, hh) for G in range(4) for hh in range(2)]
                    d_scores(*units[0])
                    for n_, (G, hh) in enumerate(units):
                        if n_ + 1 < len(units):
                            d_scores(*units[n_ + 1])
                        d_softmax(G, hh)
                        d_pv(G, hh)
                        if hh == 1:
                            d_merge(G)
                S.add("dve", lambda e, o=accD[:, :], i=accD[:, :]: e.reciprocal(out=o, in_=i), reads=["accD"], writes=["accD"])
                for g in range(4):
                    c0, n = TG[g]
                    tt("dve", xn[:, hp, c0:c0 + n], accN[:, c0:c0 + n], accD[:, c0:c0 + n], ALU.mult, ["accN", "accD"],
                       [("xn", hp, g)])

            S.enabled = ('E' in STAGES)
            S.fence()
            for half in range(2):
                wv, wk = wload(w_out[l][:, half * 512:(half + 1) * 512], 1, 8)
                for mc4 in range(4):
                    mc = half * 4 + mc4
                    for g, (c0, n) in enumerate(TG):
                        bnk = nextbank()
                        for kc in range(8):
                            mm(pbs[bnk][:, 0:n], wv[:, kc, mc4 * 128:(mc4 + 1) * 128], xn[:, kc, c0:c0 + n], kc == 0, kc == 7,
                               [("xn", kc, g)] + wk, [("pb", bnk)])
                        tt("dve", hT[:, mc, c0:c0 + n], hT[:, mc, c0:c0 + n], pbs[bnk][:, 0:n], ALU.add, [("pb", bnk), ("hT", mc, g)], [("hT", mc, g)])

            S.enabled = ('F' in STAGES)
            S.fence()
            norm_to_xn(vb + 8)
            a1 = [aview(i * 8192, [128, 8, 512], BF16) for i in range(2)]
            a2 = [aview(16 * KB + i * 8192, [128, 8, 512], BF16) for i in range(2)]
            ai = 0
            for qi in range(4):
                wu, wuk = wload(w_up[l][:, qi * 1024:(qi + 1) * 1024], 2, 8)
                wd, wdk = wload(w_down[l][qi * 1024:(qi + 1) * 1024, :], 2, 8)
                for g, (c0, n) in enumerate(TG):
                    A1, A2 = a1[ai % 2], a2[ai % 2]
                    k1, k2 = ("a1", ai % 2), ("a2", ai % 2)
                    ai += 1
                    for hc in range(8):
                        bnk = nextbank()
                        for kc in range(8):
                            mm(pbs[bnk][:, 0:n], wu[:, kc, hc * 128:(hc + 1) * 128], xn[:, kc, c0:c0 + n], kc == 0, kc == 7,
                               [("xn", kc, g)] + wuk, [("pb", bnk)])
                        act(A1[:, hc, 0:n], pbs[bnk][:, 0:n], AF.Relu, [("pb", bnk)], [k1 + (hc,)])
                        tt("dve", A2[:, hc, 0:n], A1[:, hc, 0:n], A1[:, hc, 0:n], ALU.mult, [k1 + (hc,)], [k2 + (hc,)])
                    for mc in range(8):
                        bnk = nextbank()
                        for hc in range(8):
                            mm(pbs[bnk][:, 0:n], wd[:, hc, mc * 128:(mc + 1) * 128], A2[:, hc, 0:n], hc == 0, hc == 7,
                               [k2 + (hc,)] + wdk, [("pb", bnk)])
                        tt("dve", hT[:, mc, c0:c0 + n], hT[:, mc, c0:c0 + n], pbs[bnk][:, 0:n], ALU.add, [("pb", bnk), ("hT", mc, g)], [("hT", mc, g)])

            S.enabled = ('G' in STAGES)
            S.fence()
            norm_to_xn(vb + 16)
            pTb = aview(0, [128, 2, NT], BF16)
            sg = [aview(9 * KB + i * 2048, [128, 512], F32) for i in range(2)]
            for hh_ in range(2):
                dma("pool", pTb[:, :, hh_ * 1026:(hh_ + 1) * 1026], pT[l][:, hh_ * 1026:(hh_ + 1) * 1026].rearrange("(k p) n -> p k n", p=128), [], [("pTb", hh_)])
            wpl, wplk = wload(w_ple[l], 1, 2)
            sgi = 0
            for half in range(2):
                wv, wk = wload(w_gate[l][:, half * 512:(half + 1) * 512], 1, 8)
                for mc4 in range(4):
                    mc = half * 4 + mc4
                    for g, (c0, n) in enumerate(TG):
                        bg = nextbank(0, 3)
                        bp = 3 + nextbank(0, 3)
                        for kc in range(8):
                            mm(pbs[bg][:, 0:n], wv[:, kc, mc4 * 128:(mc4 + 1) * 128], xn[:, kc, c0:c0 + n], kc == 0, kc == 7,
                               [("xn", kc, g)] + wk, [("pb", bg)])
                        for kc in range(2):
                            mm(pbs[bp][:, 0:n], wpl[:, kc, mc * 128:(mc + 1) * 128], pTb[:, kc, c0:c0 + n], kc == 0, kc == 1,
                               [("pTb", 0), ("pTb", 1)] + wplk, [("pb", bp)])
                        sgt = sg[sgi % 2]
                        sgk = ("sg", sgi % 2)
                        sgi += 1
                        act(sgt[:, 0:n], pbs[bg][:, 0:n], AF.Sigmoid, [("pb", bg)], [sgk])
                        tt("dve", sgt[:, 0:n], sgt[:, 0:n], pbs[bp][:, 0:n], ALU.mult, [sgk, ("pb", bp)], [sgk])
                        tt("dve", hT[:, mc, c0:c0 + n], hT[:, mc, c0:c0 + n], sgt[:, 0:n], ALU.add, [sgk, ("hT", mc, g)], [("hT", mc, g)])

        S.enabled = True
        S.fence()
        yst = [aview(i * 2048, [128, 512], F32) for i in range(4)]
        yi = {"i": 0}

        def fin_out(kc, g, c0, n, gcol):
            y = yst[yi["i"] % 4]
            yk = ("yst", yi["i"] % 4)
            yi["i"] += 1
            stt("dve", y[:, 0:n], hT[:, kc, c0:c0 + n], gcol, rs[:, 0:n], ALU.mult, ALU.mult, [("hT", kc, g), "rs", "vecs"], [yk])
            dma("sp", yT[kc * 128:(kc + 1) * 128, c0:c0 + n], y[:, 0:n], [yk], [])
        rmsnorm(112, fin_out)
        S.finish()
        S.emit(st)
        print('MK stats', S.stats, flush=True)
    return nc


_NC_CACHE = {}


def _host_inputs(x_prompt, x_sample, cache_attn_kv, state_pool, p_prompt, p_sample,
                 norm_attn_g, w_in, pool_w, pool_scale, w_out, norm_mlp_g, w_up, w_down,
                 ple_norm_g, w_ple_gate, w_ple, final_norm_g):
    f32 = np.float32
    A = lambda a: np.ascontiguousarray(np.asarray(a))
    x_prompt, x_sample, p_prompt, p_sample = A(x_prompt), A(x_sample), A(p_prompt), A(p_sample)
    cache_attn_kv, state_pool = np.asarray(cache_attn_kv), A(state_pool)
    vecs = np.zeros((128, 120), f32)
    fm = lambda v: np.asarray(v, f32).reshape(-1, 128).T
    for l in range(L):
        vecs[:, l * 28:l * 28 + 8] = fm(norm_attn_g[l])
        vecs[:, l * 28 + 8:l * 28 + 16] = fm(norm_mlp_g[l])
        vecs[:, l * 28 + 16:l * 28 + 24] = fm(ple_norm_g[l])
        vecs[:, l * 28 + 24:l * 28 + 28] = fm(pool_scale[l])
    vecs[:, 112:120] = fm(final_norm_g)
    k = np.arange(128)[:, None]
    q = np.arange(128)[None, :]
    mA = (k >= q).astype(f32)
    mB = (k <= q).astype(f32)
    ident = np.eye(128, dtype=f32)
    smallc = np.zeros((16, 548), f32)
    for b in range(4):
        smallc[b, b * 8:(b + 1) * 8] = 3.0
        smallc[b, 32 + b * 128:32 + (b + 1) * 128] = 1.0
    for g, w in enumerate(POOLW):
        smallc[16 - w:15, 544 + g] = 1.0 / w
    shared = dict(w_in=A(w_in), pool_w=A(pool_w), w_out=A(w_out), w_up=A(w_up), w_down=A(w_down),
                  w_gate=A(w_ple_gate), w_ple=A(w_ple), vecs=vecs, smallc=smallc, ident=ident)
    in_maps = []
    for c in range(NCORE):
        b, j = c // 4, c % 4
        first = (j == 0)
        xT = np.empty((D, NT), f32)
        xT[:, :NP] = x_prompt[b, j * NP:(j + 1) * NP].T
        xT[:, NP:] = x_sample[4 * c:4 * c + 4, 0].T
        pT = np.empty((L, 256, NT), f32)
        pT[:, :, :NP] = p_prompt[:, b, j * NP:(j + 1) * NP].transpose(0, 2, 1)
        pT[:, :, NP:] = p_sample[:, 4 * c:4 * c + 4, 0].transpose(0, 2, 1)
        cache_c = np.ascontiguousarray(cache_attn_kv[:, 4 * c:4 * c + 4]).reshape(L, NS, 2048, 1024)
        state_c = np.ascontiguousarray(state_pool[:, 4 * c:4 * c + 4])
        mE = mA * (0.0 if first else 1.0)
        masks = np.stack([np.tile(mA, (1, 4)), np.concatenate([mE, mA, mA, mA], 1), np.tile(mE, (1, 4)), np.tile(mB, (1, 4))], 1)
        cst = np.zeros((128, 80), f32)
        for g, w in enumerate(POOLW):
            t = np.arange(16)
            cst[:, g * 16:(g + 1) * 16] = (1.0 / np.minimum(t + 1, w)) if first else (1.0 / w)
        cst[:, 64:80] = 0.0 if first else 1.0
        sec = c if first else c - 1
        p = np.arange(128)[:, None]
        idx = (sec * 1024 + np.arange(8)[None, :] * 128 + p).astype(np.int32)
        idxu = (sec * 512 + np.arange(4)[None, :] * 128 + p).astype(np.int32)
        m = dict(shared)
        m.update(xT=xT, pT=pT, cache=cache_c, state=state_c, masks=np.ascontiguousarray(masks.astype(f32)), cst=cst,
                 idx=idx, idxu=idxu)
        in_maps.append(m)
    return in_maps


def kernel(**inputs):
    in_maps = _host_inputs(**inputs)
    if "nc" not in _NC_CACHE:
        _NC_CACHE["nc"] = build()
    nc = _NC_CACHE["nc"]
    res = run_bass_kernel_spmd(nc, in_maps, core_ids=list(range(NCORE))).results
    f32 = np.float32
    y_prompt = np.empty((2, 4 * NP, D), f32)
    y_sample = np.empty((32, 1, D), f32)
    kv_prompt = np.empty((L, 2, NP, 2, 8, 64), f32)
    kv_sample = np.empty((L, 32, 1, 2, 8, 64), f32)
    pool_prompt = np.empty((L, 2, 15, 512), f32)
    pool_sample = np.empty((L, 32, 15, 512), f32)
    for c in range(NCORE):
        b, j = c // 4, c % 4
        r = res[c]
        yT = np.asarray(r["yT"])
        y_prompt[b, j * NP:(j + 1) * NP] = yT[:, :NP].T
        y_sample[4 * c:4 * c + 4, 0] = yT[:, NP:].T
        kv_sample[:, 4 * c:4 * c + 4, 0] = np.asarray(r["kvs"]).reshape(L, NS, 2, 8, 64)
        pool_sample[:, 4 * c:4 * c + 4] = np.asarray(r["pools"])
        if j == 3:
            kvT = np.asarray(r["kvT"])
            kv_prompt[:, b] = kvT.reshape(L, 2, 8, 64, NP).transpose(0, 4, 1, 2, 3)
            pool_prompt[:, b] = np.asarray(r["poolp"])[:, :, 1:16].transpose(0, 2, 1)
    return (y_prompt, y_sample, kv_prompt, kv_sample, pool_prompt, pool_sample)
```
